# Optimizing a Trainium2 kernel written in Bass

```python
import jax, jax.numpy as jnp
from jax import lax
import numpy as np

D_MODEL = 1024
BATCH = 4
SEQ = 8192
DEPTH = 2

HEAD_DIM = 64
CONV_CH = 256
CONV_WIDTH = 3
HGRN_HEADS = 4
HGRN_DK = 64
HGRN_DV = 64
HGRN_QK = HGRN_HEADS * HGRN_DK
HGRN_WIDTH = HGRN_HEADS * HGRN_DV
CHUNK = 64
FOX_HEADS = 8
FOX_WIDTH = FOX_HEADS * HEAD_DIM
Q_BLOCK = 128
MIX_WIDTH = CONV_CH + HGRN_WIDTH + FOX_WIDTH
MIX_IN_SIZES = (CONV_CH, CONV_CH, CONV_CH,
                HGRN_QK, HGRN_QK, HGRN_WIDTH, HGRN_WIDTH,
                FOX_WIDTH, FOX_WIDTH, FOX_WIDTH, FOX_HEADS)
MIX_IN = sum(MIX_IN_SIZES)
D_FF = 2816
EPS = 1e-6
MASK_VALUE = -1e30

kernel_name = "hybrid_parallel_conv_hgrn2_fox_macaron"


def rms_norm(x, gain):
    xf = x.astype(jnp.float32)
    inv = lax.rsqrt(jnp.mean(xf * xf, axis=-1, keepdims=True) + EPS)
    return (xf * inv).astype(x.dtype) * gain


def swiglu(h, w_in, w_out):
    gate, up = jnp.split(h @ w_in, 2, axis=-1)
    return (jax.nn.silu(gate) * up) @ w_out


def short_conv_mixer(x_in, b_gate, c_gate, conv_w):
    u = c_gate * x_in
    taps = conv_w[:, None, :].astype(u.dtype)
    y = lax.conv_general_dilated(u, taps, window_strides=(1,),
                                 padding=((CONV_WIDTH - 1, 0),),
                                 dimension_numbers=('NWC', 'WIO', 'NWC'),
                                 feature_group_count=CONV_CH)
    return b_gate * y


def hgrn2_mixer(q, f_logit, v, g, lb, out_gain):
    f32 = jnp.float32
    bsz, seqlen, _ = q.shape
    n_chunks = seqlen // CHUNK
    z = f_logit.astype(f32)
    lb = lb.astype(f32)
    log_f = jax.nn.log_sigmoid(z) + jnp.log1p(lb * jnp.exp(-z))
    k = (1.0 - lb) * jax.nn.sigmoid(-z)

    def to_chunks(t, d):
        return t.astype(f32).reshape(bsz, n_chunks, CHUNK, HGRN_HEADS, d).transpose(1, 0, 3, 2, 4)

    qc, kc, lfc = to_chunks(q, HGRN_DK), to_chunks(k, HGRN_DK), to_chunks(log_f, HGRN_DK)
    vc = to_chunks(v, HGRN_DV)
    causal = jnp.tril(jnp.ones((CHUNK, CHUNK), dtype=bool))[:, :, None]

    def step(state, inp):
        qb, kb, vb, lfb = inp
        bcum = jnp.cumsum(lfb, axis=-2)
        rel = bcum[..., :, None, :] - bcum[..., None, :, :]
        decay = jnp.where(causal, jnp.exp(jnp.where(causal, rel, 0.0)), 0.0)
        scores = jnp.einsum('bhtd,bhsd,bhtsd->bhts', qb, kb, decay)
        o = (jnp.einsum('bhts,bhsv->bhtv', scores, vb)
             + jnp.einsum('bhtd,bhdv->bhtv', qb * jnp.exp(bcum), state))
        b_last = bcum[..., -1:, :]
        state = (jnp.exp(b_last[..., 0, :])[..., None] * state
                 + jnp.einsum('bhsd,bhsv->bhdv', kb * jnp.exp(b_last - bcum), vb))
        return state, o

    s0 = jnp.zeros((bsz, HGRN_HEADS, HGRN_DK, HGRN_DV), f32)
    _, o = lax.scan(step, s0, (qc, kc, vc, lfc))
    o = o.transpose(1, 0, 3, 2, 4).reshape(bsz, seqlen, HGRN_HEADS, HGRN_DV)
    gate = g.astype(f32).reshape(bsz, seqlen, HGRN_HEADS, HGRN_DV)
    o = rms_norm(o, out_gain) * jax.nn.silu(gate)
    return o.reshape(bsz, seqlen, HGRN_WIDTH).astype(q.dtype)


def fox_mixer(q, k, v, f_logit, f_bias, q_gain, k_gain):
    f32 = jnp.float32
    bsz, seqlen, _ = q.shape

    def heads(t):
        return t.reshape(bsz, seqlen, FOX_HEADS, HEAD_DIM).transpose(0, 2, 1, 3)

    qh = rms_norm(heads(q), q_gain) * (HEAD_DIM ** -0.5)
    kh = rms_norm(heads(k), k_gain)
    vh = heads(v)
    log_f = jax.nn.log_sigmoid((f_logit + f_bias).astype(f32))
    cum = jnp.cumsum(log_f, axis=1).transpose(0, 2, 1)
    kpos = jnp.arange(seqlen)

    def block(i):
        start = i * Q_BLOCK
        qb = lax.dynamic_slice_in_dim(qh, start, Q_BLOCK, axis=2)
        cb = lax.dynamic_slice_in_dim(cum, start, Q_BLOCK, axis=2)
        qpos = start + jnp.arange(Q_BLOCK)
        s = (jnp.einsum('bhqd,bhkd->bhqk', qb, kh).astype(f32)
             + (cb[..., :, None] - cum[..., None, :]))
        s = jnp.where(qpos[:, None] >= kpos[None, :], s, MASK_VALUE)
        p = jax.nn.softmax(s, axis=-1)
        return jnp.einsum('bhqk,bhkd->bhqd', p.astype(vh.dtype), vh)

    o = lax.map(block, jnp.arange(seqlen // Q_BLOCK))
    return o.transpose(1, 0, 3, 2, 4).reshape(bsz, seqlen, FOX_WIDTH)


def setup_inputs(seed: int = 0) -> dict:
    key = jax.random.key(seed)
    ks = jax.random.split(key, 17)
    nrm = jax.random.normal
    f32 = jnp.float32
    return {
        "x": nrm(ks[0], (BATCH, SEQ, D_MODEL), f32),
        "ffn1_norm": 1.0 + 0.02 * nrm(ks[1], (DEPTH, D_MODEL), f32),
        "ffn1_w_in": nrm(ks[2], (DEPTH, D_MODEL, 2 * D_FF), f32) * D_MODEL ** -0.5,
        "ffn1_w_out": nrm(ks[3], (DEPTH, D_FF, D_MODEL), f32) * D_FF ** -0.5,
        "mix_norm": 1.0 + 0.02 * nrm(ks[4], (DEPTH, D_MODEL), f32),
        "w_mix_in": nrm(ks[5], (DEPTH, D_MODEL, MIX_IN), f32) * D_MODEL ** -0.5,
        "conv_w": nrm(ks[6], (DEPTH, CONV_WIDTH, CONV_CH), f32) * CONV_WIDTH ** -0.5,
        "hgrn_lb_logits": nrm(ks[7], (DEPTH, HGRN_QK), f32),
        "hgrn_out_gain": 1.0 + 0.02 * nrm(ks[8], (DEPTH, HGRN_DV), f32),
        "fox_q_gain": 1.0 + 0.02 * nrm(ks[9], (DEPTH, HEAD_DIM), f32),
        "fox_k_gain": 1.0 + 0.02 * nrm(ks[10], (DEPTH, HEAD_DIM), f32),
        "fox_f_bias": 2.0 + 0.5 * nrm(ks[11], (DEPTH, FOX_HEADS), f32),
        "w_mix_out": nrm(ks[12], (DEPTH, MIX_WIDTH, D_MODEL), f32) * MIX_WIDTH ** -0.5,
        "ffn2_norm": 1.0 + 0.02 * nrm(ks[13], (DEPTH, D_MODEL), f32),
        "ffn2_w_in": nrm(ks[14], (DEPTH, D_MODEL, 2 * D_FF), f32) * D_MODEL ** -0.5,
        "ffn2_w_out": nrm(ks[15], (DEPTH, D_FF, D_MODEL), f32) * D_FF ** -0.5,
    }


def reference(x, ffn1_norm, ffn1_w_in, ffn1_w_out, mix_norm, w_mix_in, conv_w,
              hgrn_lb_logits, hgrn_out_gain, fox_q_gain, fox_k_gain, fox_f_bias,
              w_mix_out, ffn2_norm, ffn2_w_in, ffn2_w_out):
    lb_soft = jax.nn.softmax(hgrn_lb_logits.astype(jnp.float32), axis=0)
    lower_bounds = jnp.clip(jnp.cumsum(lb_soft, axis=0) - lb_soft[0:1], 0.0, 1.0)
    split_at = [int(s) for s in np.cumsum(MIX_IN_SIZES)[:-1]]
    for l in range(DEPTH):
        x = x + 0.5 * swiglu(rms_norm(x, ffn1_norm[l]), ffn1_w_in[l], ffn1_w_out[l])
        h = rms_norm(x, mix_norm[l]) @ w_mix_in[l]
        (c_x, c_b, c_c, h_q, h_f, h_i, h_g, f_q, f_k, f_v, f_f) = jnp.split(h, split_at, axis=-1)
        y = jnp.concatenate([
            short_conv_mixer(c_x, c_b, c_c, conv_w[l]),
            hgrn2_mixer(h_q, h_f, h_i, h_g, lower_bounds[l], hgrn_out_gain[l]),
            fox_mixer(f_q, f_k, f_v, f_f, fox_f_bias[l], fox_q_gain[l], fox_k_gain[l]),
        ], axis=-1)
        x = x + y @ w_mix_out[l]
        x = x + 0.5 * swiglu(rms_norm(x, ffn2_norm[l]), ffn2_w_in[l], ffn2_w_out[l])
    return x
```

```python
import contextlib
import numpy as np
import concourse.bass as bass
import concourse.mybir as mybir
from concourse.bass_utils import run_bass_kernel_spmd

F32 = mybir.dt.float32
BF16 = mybir.dt.bfloat16
AF = mybir.ActivationFunctionType
ALU = mybir.AluOpType

D_MODEL = 1024
BATCH = 4
SEQ = 8192
DEPTH = 2
D_FF = 2816
MIX_IN = 3336
EPS = 1e-6
NCORES = 8
TOK = BATCH * SEQ // NCORES
TT = 512
KC = D_MODEL // 128
FC = D_FF // 128


class Res:
    __slots__ = ("w", "r")

    def __init__(self):
        self.w = None
        self.r = {}


class Prog:
    def __init__(self, nc, n_dma=12):
        self.nc = nc
        self.engs = {"pe": nc.tensor, "act": nc.scalar, "dve": nc.vector,
                     "pool": nc.gpsimd, "sp": nc.sync}
        self.sem = {k: nc.alloc_semaphore(name=f"s_{k}") for k in ("pe", "act", "dve", "pool")}
        self.cnt = {k: 0 for k in self.sem}
        self.seen = {k: {} for k in self.engs}
        self.dq = {}
        for q in ("sp", "pool", "act"):
            self.dq[q] = {"sems": [nc.alloc_semaphore(name=f"d_{q}{i}") for i in range(n_dma)],
                          "vals": [0] * n_dma, "i": 0}
        self.out_toks = []

    def R(self, n=None):
        if n is None:
            return Res()
        return [Res() for _ in range(n)]

    def _deps(self, reads, writes):
        d = []
        for r in reads:
            if r.w is not None:
                d.append(r.w)
        for w in writes:
            if w.w is not None:
                d.append(w.w)
            d.extend(w.r.values())
        return d

    def _wait(self, ename, deps):
        e = self.engs[ename]
        seen = self.seen[ename]
        best = {}
        own = id(self.sem["pe"]) if ename == "pe" else None
        for (sem, val) in deps:
            k = id(sem)
            if k == own:
                continue
            if seen.get(k, 0) >= val:
                continue
            if k not in best or best[k][1] < val:
                best[k] = (sem, val)
        for k, (sem, val) in best.items():
            e.wait_ge(sem, val)
            seen[k] = val

    def _commit(self, tok, reads, writes):
        k = id(tok[0])
        for r in reads:
            r.r[k] = tok
        for w in writes:
            w.w = tok
            w.r = {}

    def op(self, ename, fn, reads=(), writes=()):
        self._wait(ename, self._deps(reads, writes))
        ins = fn(self.engs[ename])
        self.cnt[ename] += 1
        ins.then_inc(self.sem[ename], 1)
        tok = (self.sem[ename], self.cnt[ename])
        self._commit(tok, reads, writes)
        return tok

    def group(self, ename, fns, reads=(), writes=()):
        self._wait(ename, self._deps(reads, writes))
        e = self.engs[ename]
        ins = None
        for fn in fns:
            ins = fn(e)
        self.cnt[ename] += 1
        ins.then_inc(self.sem[ename], 1)
        tok = (self.sem[ename], self.cnt[ename])
        self._commit(tok, reads, writes)
        return tok

    def dma(self, q, out, in_, reads=(), writes=(), is_output=False):
        d = self.dq[q]
        i = d["i"]
        d["i"] = (i + 1) % len(d["sems"])
        sem = d["sems"][i]
        deps = self._deps(reads, writes)
        if d["vals"][i] > 0:
            deps.append((sem, d["vals"][i]))
        self._wait(q, deps)
        ins = self.engs[q].dma_start(out=out, in_=in_)
        ins.then_inc(sem, 16)
        d["vals"][i] += 16
        tok = (sem, d["vals"][i])
        self._commit(tok, reads, writes)
        if is_output:
            self.out_toks.append(tok)
        return tok

    def barrier(self):
        deps = [(self.sem[k], self.cnt[k]) for k in self.sem if self.cnt[k] > 0]
        for q, d in self.dq.items():
            for sem, v in zip(d["sems"], d["vals"]):
                if v > 0:
                    deps.append((sem, v))
        for en in self.engs:
            self._wait(en, deps)

    def finish(self):
        deps = []
        for q, d in self.dq.items():
            for sem, v in zip(d["sems"], d["vals"]):
                if v > 0:
                    deps.append((sem, v))
        self._wait("sp", deps)


def _mm(out, lhsT, rhs, start, stop):
    return lambda e: e.matmul(out, lhsT, rhs, start=start, stop=stop)


class Consts:
    def __init__(self, P, es):
        nc = P.nc
        self.ones_bf = es.enter_context(nc.sbuf_tensor("c_ones_bf", [128, 128], BF16))
        self.eps = es.enter_context(nc.sbuf_tensor("c_eps", [128, 1], F32))
        self.one = es.enter_context(nc.sbuf_tensor("c_one", [128, 1], F32))
        self.nln8 = es.enter_context(nc.sbuf_tensor("c_nln8", [128, 1], F32))
        self.r = P.R()
        P.op("pool", lambda e: e.memset(self.nln8[:], -2.0794415416798357), writes=[self.r])
        P.op("pool", lambda e: e.memset(self.ones_bf[:], 1.0), writes=[self.r])
        P.op("pool", lambda e: e.memset(self.eps[:], EPS), writes=[self.r])
        P.op("pool", lambda e: e.memset(self.one[:], 1.0), writes=[self.r])


def load_cast_weight(P, es, name, w_dram_view, nk, ncols, stage, stage_res, cast_engs, qs):
    nc = P.nc
    wt = es.enter_context(nc.sbuf_tensor(name, [128, nk, ncols], BF16))
    res = P.R()
    SW = stage[0].shape[1]
    idx = 0
    toks = []
    for k in range(nk):
        for c0 in range(0, ncols, SW):
            cw = min(SW, ncols - c0)
            b = idx % len(stage)
            ce = cast_engs[idx % len(cast_engs)]
            q = qs[idx % len(qs)]
            P.dma(q, stage[b][:, 0:cw], w_dram_view[:, k, c0:c0 + cw], writes=[stage_res[b]])
            sub = Res()
            if ce == "act":
                P.op(ce, lambda e, b=b, k=k, c0=c0, cw=cw: e.copy(wt[:, k, c0:c0 + cw], stage[b][:, 0:cw]),
                     reads=[stage_res[b]], writes=[sub])
            else:
                P.op(ce, lambda e, b=b, k=k, c0=c0, cw=cw: e.tensor_copy(wt[:, k, c0:c0 + cw], stage[b][:, 0:cw]),
                     reads=[stage_res[b]], writes=[sub])
            toks.append(sub)
            idx += 1
    return wt, toks


def rmsnorm_tile(P, C, xT, xT_res, gain, gain_res, sq, sq_res, xn, xn_res, ps_ss, ps_ss_res,
                 rstd, rstd_res, ncols=TT):
    P.op("act", lambda e: e.activation(sq[:, :, :], xT[:, :, :], AF.Square),
         reads=[xT_res], writes=[sq_res])
    fns = [_mm(ps_ss[:, 0:ncols], C.ones_bf[:, :], sq[:, k, :], k == 0, k == KC - 1) for k in range(KC)]
    P.group("pe", fns, reads=[sq_res, C.r], writes=[ps_ss_res])
    P.op("act", lambda e: e.activation(rstd[:, :], ps_ss[:, 0:ncols], AF.Ln, bias=C.eps[:, :], scale=1.0 / D_MODEL),
         reads=[ps_ss_res, C.r], writes=[rstd_res])
    P.op("act", lambda e: e.activation(rstd[:, :], rstd[:, :], AF.Exp, scale=-0.5),
         reads=[rstd_res], writes=[rstd_res])
    for k in range(KC):
        P.op("dve", lambda e, k=k: e.scalar_tensor_tensor(xn[:, k, :], xT[:, k, :], gain[:, k:k + 1], rstd[:, :],
                                                          ALU.mult, ALU.mult),
             reads=[xT_res, rstd_res, gain_res], writes=[xn_res[k]])


def ffn_phase(P, C, x_in, x_out, gain_d, w_in_d, w_out_d, ntok, tag, out_is_output=False):
    nc = P.nc
    nt = ntok // TT
    with contextlib.ExitStack() as es:
        SW = 1408
        stage = [es.enter_context(nc.sbuf_tensor(f"{tag}_st{i}", [128, SW], F32)) for i in range(2)]
        stage_res = P.R(2)
        w_in, w_in_r = load_cast_weight(P, es, f"{tag}_win", w_in_d.rearrange("(k p) n -> p k n", p=128),
                                        KC, 2 * D_FF, stage, stage_res, ["pool", "dve", "act"], ["sp"])
        w_out, w_out_r = load_cast_weight(P, es, f"{tag}_wout", w_out_d.rearrange("(c p) n -> p c n", p=128),
                                          FC, D_MODEL, stage, stage_res, ["pool", "dve", "act"], ["sp"])
        gain = es.enter_context(nc.sbuf_tensor(f"{tag}_gain", [128, KC], F32))
        gain_res = P.R()
        P.dma("sp", gain[:, :], gain_d, writes=[gain_res])

        xT = es.enter_context(nc.sbuf_tensor(f"{tag}_xT", [128, KC, TT], F32))
        xT_res = P.R()
        xn = es.enter_context(nc.sbuf_tensor(f"{tag}_xn", [128, KC, TT], BF16))
        xn_res = P.R(KC)
        hT = es.enter_context(nc.sbuf_tensor(f"{tag}_hT", [128, FC, TT], BF16))
        hT_res = P.R(FC)
        rstd = es.enter_context(nc.sbuf_tensor(f"{tag}_rstd", [128, TT], F32))
        rstd_res = P.R()
        sg = [es.enter_context(nc.sbuf_tensor(f"{tag}_sg{i}", [128, TT], F32)) for i in range(2)]
        sg_res = P.R(2)
        ps_g = es.enter_context(nc.psum_tensor(f"{tag}_psg", [128, 2, TT], F32))
        ps_u = es.enter_context(nc.psum_tensor(f"{tag}_psu", [128, 2, TT], F32))
        ps_o = es.enter_context(nc.psum_tensor(f"{tag}_pso", [128, 2, TT], F32))
        ps_s = es.enter_context(nc.psum_tensor(f"{tag}_pss", [128, TT], F32))
        ps_g_res, ps_u_res, ps_o_res = P.R(2), P.R(2), P.R(2)
        ps_s_res = P.R()
        sq = hT[:, 0:KC, :]
        x_in_v = x_in.rearrange("(k p) t -> p k t", p=128)
        x_out_v = x_out.rearrange("(k p) t -> p k t", p=128)
        all_w_in = w_in_r
        all_w_out = w_out_r
        for t in range(nt):
            ts = slice(t * TT, (t + 1) * TT)
            P.dma("sp", xT[:, :, :], x_in_v[:, :, ts], writes=[xT_res])
            sq_res_all = hT_res[0:KC]
            P.op("act", lambda e: e.activation(sq, xT[:, :, :], AF.Square),
                 reads=[xT_res], writes=sq_res_all)
            fns = [_mm(ps_s[:, :], C.ones_bf[:, :], sq[:, k, :], k == 0, k == KC - 1) for k in range(KC)]
            P.group("pe", fns, reads=list(sq_res_all) + [C.r], writes=[ps_s_res])
            P.op("act", lambda e: e.activation(rstd[:, :], ps_s[:, :], AF.Ln, bias=C.eps[:, :], scale=1.0 / D_MODEL),
                 reads=[ps_s_res, C.r], writes=[rstd_res])
            P.op("act", lambda e: e.activation(rstd[:, :], rstd[:, :], AF.Exp, scale=-0.5),
                 reads=[rstd_res], writes=[rstd_res])
            for k in range(KC):
                P.op("dve", lambda e, k=k: e.scalar_tensor_tensor(xn[:, k, :], xT[:, k, :], gain[:, k:k + 1],
                                                                  rstd[:, :], ALU.mult, ALU.mult),
                     reads=[xT_res, rstd_res, gain_res], writes=[xn_res[k]])
            for c in range(FC):
                b = c % 2
                fg = [_mm(ps_g[:, b, :], w_in[:, k, c * 128:(c + 1) * 128], xn[:, k, :], k == 0, k == KC - 1)
                      for k in range(KC)]
                P.group("pe", fg, reads=list(xn_res) + (all_w_in if t == 0 else []), writes=[ps_g_res[b]])
                fu = [_mm(ps_u[:, b, :], w_in[:, k, D_FF + c * 128:D_FF + (c + 1) * 128], xn[:, k, :], k == 0,
                          k == KC - 1) for k in range(KC)]
                P.group("pe", fu, reads=list(xn_res), writes=[ps_u_res[b]])
                P.op("act", lambda e, b=b: e.activation(sg[b][:, :], ps_g[:, b, :], AF.Silu),
                     reads=[ps_g_res[b]], writes=[sg_res[b]])
                P.op("dve", lambda e, b=b, c=c: e.tensor_tensor(hT[:, c, :], sg[b][:, :], ps_u[:, b, :], ALU.mult),
                     reads=[sg_res[b], ps_u_res[b]], writes=[hT_res[c]])
            for m in range(KC):
                b = m % 2
                fo = [_mm(ps_o[:, b, :], w_out[:, c, m * 128:(m + 1) * 128], hT[:, c, :], c == 0, c == FC - 1)
                      for c in range(FC)]
                P.group("pe", fo, reads=list(hT_res) + (all_w_out if t == 0 else []), writes=[ps_o_res[b]])
                P.op("dve", lambda e, b=b, m=m: e.scalar_tensor_tensor(xT[:, m, :], ps_o[:, b, :], 0.5, xT[:, m, :],
                                                                       ALU.mult, ALU.add),
                     reads=[ps_o_res[b], xT_res], writes=[xT_res])
            P.dma("sp", x_out_v[:, :, ts], xT[:, :, :], reads=[xT_res], is_output=out_is_output)
    P.barrier()


class Rot:
    def __init__(self, P, tiles):
        self.tiles = tiles
        self.res = P.R(len(tiles))
        self.i = 0

    def next(self):
        i = self.i
        self.i = (i + 1) % len(self.tiles)
        return self.tiles[i], self.res[i]


SKIP = set()


def mixin_phase(P, C, x_in, gain_d, w_mi_d, sm, outs, ntok, tag):
    nc = P.nc
    nt = ntok // TT
    with contextlib.ExitStack() as es:
        SW = 1408
        stage = [es.enter_context(nc.sbuf_tensor(f"{tag}_st{i}", [128, SW], F32)) for i in range(2)]
        stage_res = P.R(2)
        w, w_r = load_cast_weight(P, es, f"{tag}_wmi", w_mi_d.rearrange("(k p) n -> p k n", p=128),
                                  KC, MIX_IN, stage, stage_res, ["pool", "dve", "act"], ["sp"])
        gain = es.enter_context(nc.sbuf_tensor(f"{tag}_gain", [128, KC], F32))
        lbl = es.enter_context(nc.sbuf_tensor(f"{tag}_lbl", [128, 2, 2], F32))
        lb = es.enter_context(nc.sbuf_tensor(f"{tag}_lb", [128, 2], F32))
        oml = es.enter_context(nc.sbuf_tensor(f"{tag}_oml", [128, 2], F32))
        qg = es.enter_context(nc.sbuf_tensor(f"{tag}_qg", [128, 1], F32))
        kg = es.enter_context(nc.sbuf_tensor(f"{tag}_kg", [128, 1], F32))
        nfb = es.enter_context(nc.sbuf_tensor(f"{tag}_nfb", [8, 1], F32))
        bd = es.enter_context(nc.sbuf_tensor(f"{tag}_bd", [128, 128], BF16))
        sm_res = P.R()
        P.dma("sp", gain[:, :], gain_d, writes=[sm_res])
        P.dma("sp", lbl[:, :, :], sm["lbl"], writes=[sm_res])
        P.dma("sp", qg[:, :], sm["qg"], writes=[sm_res])
        P.dma("sp", kg[:, :], sm["kg"], writes=[sm_res])
        P.dma("sp", nfb[:, :], sm["fb"], writes=[sm_res])
        P.op("dve", lambda e: e.tensor_scalar(nfb[:, :], nfb[:, :], -1.0, None, ALU.mult), reads=[sm_res], writes=[sm_res])
        lflag = es.enter_context(nc.sbuf_tensor(f"{tag}_lflag", [128, 1], F32))
        P.dma("sp", lflag[:, :], sm["lflag"], writes=[sm_res])
        P.op("dve", lambda e: e.tensor_tensor(lb[:, :], lbl[:, :, 0], lbl[:, :, 1], ALU.subtract),
             reads=[sm_res], writes=[sm_res])
        P.op("act", lambda e: e.activation(lb[:, :], lb[:, :], AF.Exp), reads=[sm_res], writes=[sm_res])
        P.op("dve", lambda e: e.tensor_scalar(lb[:, :], lb[:, :], 1.0, None, ALU.add), reads=[sm_res], writes=[sm_res])
        P.op("dve", lambda e: e.reciprocal(lb[:, :], lb[:, :]), reads=[sm_res], writes=[sm_res])
        P.op("dve", lambda e: e.tensor_scalar(lb[:, :], lb[:, :], lflag[:, 0:1], None, ALU.mult), reads=[sm_res], writes=[sm_res])
        P.op("dve", lambda e: e.tensor_scalar(oml[:, :], lb[:, :], -1.0, 1.0, ALU.mult, ALU.add),
             reads=[sm_res], writes=[sm_res])
        P.op("pool", lambda e: e.memset(bd[:, :], 0.0), writes=[sm_res])
        P.op("pool", lambda e: e.memset(bd[0:64, 0:64], 1.0), reads=[sm_res], writes=[sm_res])
        P.op("pool", lambda e: e.memset(bd[64:128, 64:128], 1.0), reads=[sm_res], writes=[sm_res])

        xT = es.enter_context(nc.sbuf_tensor(f"{tag}_xT", [128, KC, TT], F32))
        xT_res = P.R()
        xn = es.enter_context(nc.sbuf_tensor(f"{tag}_xn", [128, KC, TT], BF16))
        xn_res = P.R(KC)
        sq = es.enter_context(nc.sbuf_tensor(f"{tag}_sq", [128, KC, TT], BF16))
        sq_res = P.R()
        rstd = es.enter_context(nc.sbuf_tensor(f"{tag}_rstd", [128, TT], F32))
        rstd_res = P.R()
        f32t = Rot(P, [es.enter_context(nc.sbuf_tensor(f"{tag}_f{i}", [128, TT], F32)) for i in range(8)])
        bft = Rot(P, [es.enter_context(nc.sbuf_tensor(f"{tag}_b{i}", [128, TT], BF16)) for i in range(10)])
        psall = es.enter_context(nc.psum_tensor(f"{tag}_ps", [128, 8, TT], F32))
        ps = Rot(P, [psall[:, i, :] for i in range(6)])
        ps_s, ps_s_res = psall[:, 6, :], P.R()
        ps2 = Rot(P, [psall[:, 7, :]])
        x_in_v = x_in.rearrange("(k p) t -> p k t", p=128)
        first = [True]

        def proj(c0, M):
            pt, pr = ps.next()
            fns = [_mm(pt[0:M, :], w[:, k, c0:c0 + M], xn[:, k, :], k == 0, k == KC - 1) for k in range(KC)]
            P.group("pe", fns, reads=list(xn_res) + (w_r if first[0] else []), writes=[pr])
            first[0] = False
            return pt, pr

        def store(dst, tile, tres):
            P.dma("sp", dst, tile, reads=[tres])

        for t in range(nt):
            ts = slice(t * TT, (t + 1) * TT)
            P.dma("sp", xT[:, :, :], x_in_v[:, :, ts], writes=[xT_res])
            P.op("act", lambda e: e.activation(sq[:, :, :], xT[:, :, :], AF.Square), reads=[xT_res], writes=[sq_res])
            fns = [_mm(ps_s, C.ones_bf[:, :], sq[:, k, :], k == 0, k == KC - 1) for k in range(KC)]
            P.group("pe", fns, reads=[sq_res, C.r], writes=[ps_s_res])
            P.op("act", lambda e: e.activation(rstd[:, :], ps_s, AF.Ln, bias=C.eps[:, :], scale=1.0 / D_MODEL),
                 reads=[ps_s_res, C.r], writes=[rstd_res])
            P.op("act", lambda e: e.activation(rstd[:, :], rstd[:, :], AF.Exp, scale=-0.5),
                 reads=[rstd_res], writes=[rstd_res])
            for k in range(KC):
                P.op("dve", lambda e, k=k: e.scalar_tensor_tensor(xn[:, k, :], xT[:, k, :], gain[:, k:k + 1],
                                                                  rstd[:, :], ALU.mult, ALU.mult),
                     reads=[xT_res, rstd_res, sm_res], writes=[xn_res[k]])
            for which, base, gcol, dst in ((("q", 1792, qg, outs["fq"]), ("k", 2304, kg, outs["fk"])) if "foxqk" not in SKIP else ()):
                for g in range(4):
                    pt, pr = proj(base + g * 128, 128)
                    s_t, s_r = bft.next()
                    P.op("act", lambda e, s_t=s_t, pt=pt: e.activation(s_t[:, :], pt, AF.Square), reads=[pr], writes=[s_r])
                    p2, p2r = ps2.next()
                    P.group("pe", [_mm(p2, bd[:, :], s_t[:, :], True, True)], reads=[s_r, sm_res], writes=[p2r])
                    r_t, r_r = f32t.next()
                    P.op("act", lambda e, r_t=r_t, p2=p2: e.activation(r_t[:, :], p2, AF.Ln, bias=C.eps[:, :], scale=1.0 / 64),
                         reads=[p2r, C.r], writes=[r_r])
                    if which == "q":
                        P.op("act", lambda e, r_t=r_t: e.activation(r_t[:, :], r_t[:, :], AF.Exp, bias=C.nln8[:, :], scale=-0.5),
                             reads=[r_r, C.r], writes=[r_r])
                    else:
                        P.op("act", lambda e, r_t=r_t: e.activation(r_t[:, :], r_t[:, :], AF.Exp, scale=-0.5),
                             reads=[r_r], writes=[r_r])
                    o_t, o_r = bft.next()
                    P.op("dve", lambda e, o_t=o_t, pt=pt, r_t=r_t, gcol=gcol: e.scalar_tensor_tensor(
                        o_t[:, :], pt, gcol[:, 0:1], r_t[:, :], ALU.mult, ALU.mult), reads=[pr, r_r, sm_res], writes=[o_r])
                    store(dst[g * 128:(g + 1) * 128, ts], o_t[:, :], o_r)
            if "foxf" in SKIP:
                continue
            pt, pr = proj(3328, 8)
            e_t, e_r = f32t.next()
            P.op("act", lambda e, e_t=e_t, pt=pt: e.activation(e_t[0:8, :], pt[0:8, :], AF.Exp, bias=nfb[:, :], scale=-1.0),
                 reads=[pr, sm_res], writes=[e_r])
            P.op("act", lambda e, e_t=e_t: e.activation(e_t[0:8, :], e_t[0:8, :], AF.Ln, bias=C.one[0:8, :], scale=1.0),
                 reads=[e_r, C.r], writes=[e_r])
            P.op("dve", lambda e, e_t=e_t: e.tensor_scalar(e_t[0:8, :], e_t[0:8, :], -1.0, None, ALU.mult),
                 reads=[e_r], writes=[e_r])
            store(outs["flf"][:, ts], e_t[0:8, :], e_r)
            if "hgrn" in SKIP:
                continue
            for jh in range(2):
                pt, pr = proj(1024 + jh * 128, 128)
                e_t, e_r = f32t.next()
                P.op("act", lambda e, e_t=e_t, pt=pt: e.activation(e_t[:, :], pt, AF.Exp, scale=-1.0), reads=[pr], writes=[e_r])
                a_t, a_r = f32t.next()
                P.op("act", lambda e, a_t=a_t, e_t=e_t, jh=jh: e.activation(a_t[:, :], e_t[:, :], AF.Ln, bias=C.one[:, :],
                                                                             scale=lb[:, jh:jh + 1]),
                     reads=[e_r, sm_res, C.r], writes=[a_r])
                b_t, b_r = f32t.next()
                P.op("act", lambda e, b_t=b_t, e_t=e_t: e.activation(b_t[:, :], e_t[:, :], AF.Ln, bias=C.one[:, :], scale=1.0),
                     reads=[e_r, C.r], writes=[b_r])
                P.op("dve", lambda e, a_t=a_t, b_t=b_t: e.tensor_tensor(a_t[:, :], a_t[:, :], b_t[:, :], ALU.subtract),
                     reads=[a_r, b_r], writes=[a_r])
                store(outs["hlf"][jh * 128:(jh + 1) * 128, ts], a_t[:, :], a_r)
                P.op("act", lambda e, b_t=b_t: e.activation(b_t[:, :], b_t[:, :], AF.Exp, scale=-1.0), reads=[b_r], writes=[b_r])
                k_t, k_r = bft.next()
                P.op("dve", lambda e, k_t=k_t, e_t=e_t, b_t=b_t: e.tensor_tensor(k_t[:, :], e_t[:, :], b_t[:, :], ALU.mult),
                     reads=[e_r, b_r], writes=[k_r])
                store(outs["hk"][jh * 128:(jh + 1) * 128, ts], k_t[:, :], k_r)
            for jh in range(2):
                pt, pr = proj(768 + jh * 128, 128)
                o_t, o_r = bft.next()
                P.op("dve", lambda e, o_t=o_t, pt=pt, jh=jh: e.tensor_scalar(o_t[:, :], pt, oml[:, jh:jh + 1], None, ALU.mult),
                     reads=[pr, sm_res], writes=[o_r])
                store(outs["hq"][jh * 128:(jh + 1) * 128, ts], o_t[:, :], o_r)
            if "conv" in SKIP:
                continue
            for jh in range(2):
                pa, par = proj(jh * 128, 128)
                pc, pcr = proj(512 + jh * 128, 128)
                c_t, c_r = f32t.next()
                P.op("act", lambda e, c_t=c_t, pc=pc: e.copy(c_t[:, :], pc), reads=[pcr], writes=[c_r])
                u_t, u_r = bft.next()
                P.op("dve", lambda e, u_t=u_t, pa=pa, c_t=c_t: e.tensor_tensor(u_t[:, :], pa, c_t[:, :], ALU.mult),
                     reads=[par, c_r], writes=[u_r])
                store(outs["cu"][jh * 128:(jh + 1) * 128, ts], u_t[:, :], u_r)
                pb, pbr = proj(256 + jh * 128, 128)
                b_t, b_r = bft.next()
                P.op("act", lambda e, b_t=b_t, pb=pb: e.copy(b_t[:, :], pb), reads=[pbr], writes=[b_r])
                store(outs["cb"][jh * 128:(jh + 1) * 128, ts], b_t[:, :], b_r)
            for jh in range(2):
                pt, pr = proj(1536 + jh * 128, 128)
                o_t, o_r = bft.next()
                P.op("act", lambda e, o_t=o_t, pt=pt: e.activation(o_t[:, :], pt, AF.Silu), reads=[pr], writes=[o_r])
                store(outs["hg"][jh * 128:(jh + 1) * 128, ts], o_t[:, :], o_r)
            if "vtok" in SKIP:
                continue
            for sub in range(4):
                tsub = slice(t * TT + sub * 128, t * TT + (sub + 1) * 128)
                for (c0, ncol, dst) in ((1280, 256, outs["hv"]), (2816, 512, outs["fv"])):
                    if ("hvonly" in SKIP and ncol == 512) or ("fvonly" in SKIP and ncol == 256):
                        continue
                    pt, pr = ps.next()
                    fns = [_mm(pt[:, 0:ncol], xn[:, k, sub * 128:(sub + 1) * 128], w[:, k, c0:c0 + ncol], k == 0, k == KC - 1)
                           for k in range(KC)]
                    P.group("pe", fns, reads=list(xn_res), writes=[pr])
                    o_t, o_r = bft.next()
                    if ncol == 512:
                        P.op("act", lambda e, o_t=o_t, pt=pt: e.copy(o_t[:, :], pt), reads=[pr], writes=[o_r])
                    else:
                        P.op("dve", lambda e, o_t=o_t, pt=pt, ncol=ncol: e.tensor_copy(o_t[:, 0:ncol], pt[:, 0:ncol]),
                             reads=[pr], writes=[o_r])
                    if "nostore" not in SKIP:
                        store(dst[tsub, :], o_t[:, 0:ncol], o_r)
    P.barrier()


NEG_BIG = -30000.0


def mixer_phase(P, C, d, yT, tag):
    nc = P.nc
    L = SEQ
    NT = L // TT
    with contextlib.ExitStack() as es0:
        cres = P.R()
        ones_f = es0.enter_context(nc.sbuf_tensor(f"{tag}_onesf", [128, 128], F32))
        identb = es0.enter_context(nc.sbuf_tensor(f"{tag}_identb", [128, 128], BF16))
        nbI = es0.enter_context(nc.sbuf_tensor(f"{tag}_nbI", [128, 128], BF16))
        triu = es0.enter_context(nc.sbuf_tensor(f"{tag}_triu", [128, 128], BF16))
        trif = es0.enter_context(nc.sbuf_tensor(f"{tag}_trif", [128, 128], F32))
        m01 = es0.enter_context(nc.sbuf_tensor(f"{tag}_m01", [64, 8, 64], BF16))
        rmask = es0.enter_context(nc.sbuf_tensor(f"{tag}_rmask", [64, TT], F32))
        P.op("pool", lambda e: e.memset(ones_f[:, :], 1.0), writes=[cres])
        P.op("pool", lambda e: e.affine_select(identb[:, :], C.ones_bf[:, :], [[-1, 128]], ALU.is_equal, 0.0,
                                               base=0, channel_multiplier=1), reads=[C.r], writes=[cres])
        P.op("pool", lambda e: e.tensor_scalar(nbI[:, :], identb[:, :], NEG_BIG, None, ALU.mult), reads=[cres], writes=[cres])
        P.op("pool", lambda e: e.affine_select(triu[:, :], C.ones_bf[:, :], [[1, 128]], ALU.is_ge, 0.0,
                                               base=-1, channel_multiplier=-1), reads=[C.r], writes=[cres])
        P.op("pool", lambda e: e.affine_select(trif[:, :], ones_f[:, :], [[1, 128]], ALU.is_ge, 0.0,
                                               base=0, channel_multiplier=-1), reads=[cres], writes=[cres])
        for c in range(8):
            P.op("pool", lambda e, c=c: e.affine_select(m01[:, c, :], C.ones_bf[0:64, 0:64], [[1, 64]], ALU.is_ge, 0.0,
                                                        base=0, channel_multiplier=-1), reads=[C.r, cres], writes=[cres])
        P.op("pool", lambda e: e.memset(rmask[:, :], 1.0), reads=[cres], writes=[cres])
        P.op("pool", lambda e: e.memset(rmask[:, 0:TT:64], 0.0), reads=[cres], writes=[cres])

        with contextlib.ExitStack() as es:
            cw = es.enter_context(nc.sbuf_tensor(f"{tag}_cw", [128, 3], F32))
            cw_r = P.R()
            P.dma("sp", cw[:, :], d["cw"], writes=[cw_r])
            ut = [es.enter_context(nc.sbuf_tensor(f"{tag}_ut{i}", [128, TT + 2], BF16)) for i in range(2)]
            ut_r = P.R(2)
            bt = [es.enter_context(nc.sbuf_tensor(f"{tag}_bt{i}", [128, TT], BF16)) for i in range(2)]
            bt_r = P.R(2)
            acc = [es.enter_context(nc.sbuf_tensor(f"{tag}_acc{i}", [128, TT], F32)) for i in range(2)]
            acc_r = P.R(2)
            yo = [es.enter_context(nc.sbuf_tensor(f"{tag}_yo{i}", [128, TT], BF16)) for i in range(2)]
            yo_r = P.R(2)
            for t in range(NT):
                b = t % 2
                if t == 0:
                    P.op("dve", lambda e: e.memset(ut[0][:, 0:2], 0.0), writes=[ut_r[0]])
                    P.dma("sp", ut[0][:, 2:TT + 2], d["cu"][:, 0:TT], writes=[ut_r[0]])
                else:
                    P.dma("sp", ut[b][:, :], d["cu"][:, t * TT - 2:(t + 1) * TT], writes=[ut_r[b]])
                P.dma("sp", bt[b][:, :], d["cb"][:, t * TT:(t + 1) * TT], writes=[bt_r[b]])
                P.op("dve", lambda e, b=b: e.tensor_scalar(acc[b][:, :], ut[b][:, 2:TT + 2], cw[:, 2:3], None, ALU.mult),
                     reads=[ut_r[b], cw_r], writes=[acc_r[b]])
                P.op("dve", lambda e, b=b: e.scalar_tensor_tensor(acc[b][:, :], ut[b][:, 1:TT + 1], cw[:, 1:2], acc[b][:, :],
                                                                  ALU.mult, ALU.add), reads=[ut_r[b], cw_r, acc_r[b]], writes=[acc_r[b]])
                P.op("dve", lambda e, b=b: e.scalar_tensor_tensor(acc[b][:, :], ut[b][:, 0:TT], cw[:, 0:1], acc[b][:, :],
                                                                  ALU.mult, ALU.add), reads=[ut_r[b], cw_r, acc_r[b]], writes=[acc_r[b]])
                P.op("dve", lambda e, b=b: e.tensor_tensor(yo[b][:, :], acc[b][:, :], bt[b][:, :], ALU.mult),
                     reads=[acc_r[b], bt_r[b]], writes=[yo_r[b]])
                P.dma("sp", yT[0:128, t * TT:(t + 1) * TT], yo[b][:, :], reads=[yo_r[b]], is_output=True)
        P.barrier()

        with contextlib.ExitStack() as es:
            hgain = es.enter_context(nc.sbuf_tensor(f"{tag}_hgain", [64, 1], F32))
            hgain_r = P.R()
            P.dma("sp", hgain[:, :], d["hgain"], writes=[hgain_r])

            def hgrn_head(hh):
                pre = f"{tag}_h{hh}"
                rows = slice(hh * 64, (hh + 1) * 64)
                mk = lambda n, shp, dt: es.enter_context(nc.sbuf_tensor(f"{pre}_{n}", shp, dt))
                qp, kp, gt = mk("qp", [64, TT], BF16), mk("kp", [64, TT], BF16), mk("gt", [64, TT], BF16)
                lf = mk("lf", [64, TT], F32)
                vt = mk("vt", [64, 8, 64], BF16)
                Gc, Rc, Mq = mk("Gc", [64, TT], F32), mk("Rc", [64, TT], F32), mk("Mq", [64, TT], F32)
                Ed, Ekd, Eq, Ek = mk("Ed", [64, TT], F32), mk("Ekd", [64, TT], F32), mk("Eq", [64, TT], F32), mk("Ek", [64, TT], F32)
                qd, kd, qt, kt = mk("qd", [64, TT], BF16), mk("kd", [64, TT], BF16), mk("qt", [64, TT], BF16), mk("kt", [64, TT], BF16)
                kdT = mk("kdT", [64, 8, 64], BF16)
                Asb = mk("Asb", [64, 8, 64], BF16)
                Sall = mk("Sall", [64, 9, 64], F32)
                Sbf = mk("Sbf", [64, 8, 64], BF16)
                sq = mk("sq", [64, TT], BF16)
                rr = mk("rr", [64, TT], F32)
                t1 = mk("t1", [64, TT], F32)
                yo = mk("yo", [64, TT], BF16)
                ptr = es.enter_context(nc.psum_tensor(f"{pre}_ptr", [64, 8, 64], BF16))
                pkv = es.enter_context(nc.psum_tensor(f"{pre}_pkv", [64, 8, 64], F32))
                pA = es.enter_context(nc.psum_tensor(f"{pre}_pA", [64, 8, 64], F32))
                po = es.enter_context(nc.psum_tensor(f"{pre}_po", [64, TT], F32))
                (r_in, r_v, r_G, r_R, r_M, r_Ed, r_Ekd, r_Eq, r_Ek, r_qd, r_kd, r_qt, r_kt, r_kdT, r_A, r_S, r_Sbf,
                 r_sq, r_rr, r_t1, r_yo, r_ptr, r_pkv, r_pA, r_po, r_lf, r_g) = P.R(27)
                P.op("dve", lambda e: e.memset(Sall[:, :, :], 0.0), writes=[r_S])
                yield
                for t in range(NT):
                    ts = slice(t * TT, (t + 1) * TT)
                    P.dma("sp", qp[:, :], d["hq"][rows, ts], writes=[r_in])
                    P.dma("sp", kp[:, :], d["hk"][rows, ts], writes=[r_in])
                    P.dma("sp", lf[:, :], d["hlf"][rows, ts], writes=[r_lf])
                    P.dma("sp", gt[:, :], d["hg"][rows, ts], writes=[r_g])
                    P.dma("sp", vt[:, :, :], d["hv"][ts, rows].rearrange("(c s) v -> s c v", s=64), writes=[r_v])
                    if t > 0:
                        P.op("dve", lambda e: e.tensor_copy(Sall[:, 0, :], Sall[:, 8, :]), reads=[r_S], writes=[r_S])
                    P.op("dve", lambda e: e.tensor_tensor_scan(Gc[:, :], rmask[:, :], lf[:, :], 0.0, ALU.mult, ALU.add),
                         reads=[r_lf, cres], writes=[r_G])
                    yield
                    for c in range(8):
                        cs = slice(c * 64, (c + 1) * 64)
                        P.op("dve", lambda e, cs=cs, c=c: e.tensor_scalar(Rc[:, cs], Gc[:, cs], -1.0, Gc[:, c * 64 + 63:c * 64 + 64],
                                                                          ALU.mult, ALU.add), reads=[r_G], writes=[r_R])
                        P.op("dve", lambda e, cs=cs, c=c: e.tensor_scalar(Mq[:, cs], Gc[:, cs], Gc[:, c * 64 + 31:c * 64 + 32], None,
                                                                          ALU.subtract), reads=[r_G], writes=[r_M])
                    yield
                    P.op("act", lambda e: e.activation(Ed[:, :], Gc[:, :], AF.Exp), reads=[r_G], writes=[r_Ed])
                    P.op("act", lambda e: e.activation(Ekd[:, :], Rc[:, :], AF.Exp), reads=[r_R], writes=[r_Ekd])
                    P.op("act", lambda e: e.activation(Eq[:, :], Mq[:, :], AF.Exp), reads=[r_M], writes=[r_Eq])
                    P.op("act", lambda e: e.activation(Ek[:, :], Mq[:, :], AF.Exp, scale=-1.0), reads=[r_M], writes=[r_Ek])
                    yield
                    P.op("dve", lambda e: e.tensor_tensor(kd[:, :], kp[:, :], Ekd[:, :], ALU.mult), reads=[r_in, r_Ekd], writes=[r_kd])
                    P.op("dve", lambda e: e.tensor_tensor(kt[:, :], kp[:, :], Ek[:, :], ALU.mult), reads=[r_in, r_Ek], writes=[r_kt])
                    P.op("dve", lambda e: e.tensor_tensor(qt[:, :], qp[:, :], Eq[:, :], ALU.mult), reads=[r_in, r_Eq], writes=[r_qt])
                    P.op("dve", lambda e: e.tensor_tensor(qd[:, :], qp[:, :], Ed[:, :], ALU.mult), reads=[r_in, r_Ed], writes=[r_qd])
                    yield
                    fns = [(lambda e, c=c: e.transpose(ptr[:, c, :], kd[:, c * 64:(c + 1) * 64], identb[0:64, 0:64])) for c in range(8)]
                    P.group("pe", fns, reads=[r_kd, cres], writes=[r_ptr])
                    P.op("act", lambda e: e.copy(kdT[:, :, :], ptr[:, :, :]), reads=[r_ptr], writes=[r_kdT])
                    fns = [_mm(pA[:, c, :], kt[:, c * 64:(c + 1) * 64], qt[:, c * 64:(c + 1) * 64], True, True) for c in range(8)]
                    P.group("pe", fns, reads=[r_kt, r_qt], writes=[r_pA])
                    P.op("dve", lambda e: e.tensor_tensor(Asb[:, :, :], pA[:, :, :], m01[:, :, :], ALU.mult), reads=[r_pA, cres], writes=[r_A])
                    yield
                    fns = [_mm(pkv[:, c, :], kdT[:, c, :], vt[:, c, :], True, True) for c in range(8)]
                    P.group("pe", fns, reads=[r_kdT, r_v], writes=[r_pkv])
                    for c in range(8):
                        P.op("dve", lambda e, c=c: e.scalar_tensor_tensor(Sall[:, c + 1, :], Sall[:, c, :], Ed[:, c * 64 + 63:c * 64 + 64],
                                                                          pkv[:, c, :], ALU.mult, ALU.add),
                             reads=[r_S, r_Ed, r_pkv], writes=[r_S])
                        if c % 2 == 1:
                            yield
                    P.op("act", lambda e: e.copy(Sbf[:, :, :], Sall[:, 0:8, :]), reads=[r_S], writes=[r_Sbf])
                    yield
                    fns = []
                    for c in range(8):
                        cs = slice(c * 64, (c + 1) * 64)
                        fns.append(_mm(po[:, cs], vt[:, c, :], Asb[:, c, :], True, False))
                        fns.append(_mm(po[:, cs], Sbf[:, c, :], qd[:, cs], False, True))
                    P.group("pe", fns, reads=[r_v, r_A, r_Sbf, r_qd], writes=[r_po])
                    P.op("act", lambda e: e.activation(sq[:, :], po[:, :], AF.Square), reads=[r_po], writes=[r_sq])
                    yield
                    P.group("pe", [_mm(pA[:, :, :].rearrange("p c t -> p (c t)"), C.ones_bf[0:64, 0:64], sq[:, :], True, True)],
                            reads=[r_sq, C.r, r_A], writes=[r_pA])
                    P.op("act", lambda e: e.activation(rr[:, :], pA[:, :, :].rearrange("p c t -> p (c t)"), AF.Ln, bias=C.eps[0:64, :],
                                                       scale=1.0 / 64), reads=[r_pA, C.r], writes=[r_rr])
                    P.op("act", lambda e: e.activation(rr[:, :], rr[:, :], AF.Exp, scale=-0.5), reads=[r_rr], writes=[r_rr])
                    yield
                    P.op("dve", lambda e: e.tensor_tensor(t1[:, :], po[:, :], rr[:, :], ALU.mult), reads=[r_po, r_rr], writes=[r_t1])
                    P.op("dve", lambda e: e.scalar_tensor_tensor(yo[:, :], t1[:, :], hgain[:, 0:1], gt[:, :], ALU.mult, ALU.mult),
                         reads=[r_t1, hgain_r, r_g], writes=[r_yo])
                    P.dma("sp", yT[128 + hh * 64:128 + (hh + 1) * 64, ts], yo[:, :], reads=[r_yo], is_output=True)
                    yield

            gens = [hgrn_head(0), hgrn_head(1)]
            alive = [True, True]
            while any(alive):
                for i, g in enumerate(gens):
                    if alive[i]:
                        try:
                            next(g)
                        except StopIteration:
                            alive[i] = False
        P.barrier()

        with contextlib.ExitStack() as es:
            NB = L // 128
            NQ = L // TT
            lft = es.enter_context(nc.sbuf_tensor(f"{tag}_lft", [128, NB * 4], F32))
            Ctok = es.enter_context(nc.sbuf_tensor(f"{tag}_Ctok", [128, NB, 4], F32))
            offs = es.enter_context(nc.sbuf_tensor(f"{tag}_offs", [128, NB, 4], F32))
            totb = es.enter_context(nc.sbuf_tensor(f"{tag}_totb", [128, NB, 4], F32))
            lfq = es.enter_context(nc.sbuf_tensor(f"{tag}_lfq", [64, TT], F32))
            onesq = es.enter_context(nc.sbuf_tensor(f"{tag}_onesq", [128, TT], F32))
            rbf = es.enter_context(nc.sbuf_tensor(f"{tag}_rbf", [64, TT], BF16))
            pc = es.enter_context(nc.psum_tensor(f"{tag}_pc", [128, 2, 256], F32))
            r_lft, r_C, r_offs, r_tot, r_lfq, r_rbf, r_pc, r_ones = P.R(8)
            P.dma("sp", lft[:, :], d["flf_tok"], writes=[r_lft])
            P.dma("sp", lfq[:, :], d["flf_q"], writes=[r_lfq])
            P.op("pool", lambda e: e.memset(onesq[:, :], 1.0), writes=[r_ones])
            P.group("pe", [_mm(pc[:, 0, :], trif[:, :], lft[:, :], True, True),
                           _mm(pc[:, 1, :], ones_f[:, :], lft[:, :], True, True)], reads=[r_lft, cres], writes=[r_pc])
            P.op("dve", lambda e: e.tensor_copy(totb[:, :, :].rearrange("p b h -> p (b h)"), pc[:, 1, :]), reads=[r_pc], writes=[r_tot])
            for h in range(4):
                P.op("dve", lambda e, h=h: e.tensor_tensor_scan(offs[:, :, h], onesq[:, 0:NB], totb[:, :, h], 0.0, ALU.mult, ALU.add),
                     reads=[r_tot, r_ones], writes=[r_offs])
            P.op("dve", lambda e: e.tensor_tensor(totb[:, :, :], offs[:, :, :], totb[:, :, :], ALU.subtract),
                 reads=[r_offs, r_tot], writes=[r_tot])
            P.op("dve", lambda e: e.tensor_tensor(Ctok[:, :, :].rearrange("p b h -> p (b h)"), pc[:, 0, :],
                                                  totb[:, :, :].rearrange("p b h -> p (b h)"), ALU.add),
                 reads=[r_pc, r_tot], writes=[r_C])
            rq = es.enter_context(nc.sbuf_tensor(f"{tag}_rq", [64, TT], F32))
            r_rq = P.R()
            P.op("dve", lambda e: e.tensor_tensor_scan(rq[:, :], onesq[0:64, :], lfq[:, :], 0.0, ALU.mult, ALU.add),
                 reads=[r_lfq, r_ones], writes=[r_rq])
            P.op("dve", lambda e: e.tensor_copy(rbf[:, :], rq[:, :]), reads=[r_rq], writes=[r_rbf])

            Ka = [es.enter_context(nc.sbuf_tensor(f"{tag}_Ka{i}", [65, L], BF16)) for i in range(2)]
            Qa = [es.enter_context(nc.sbuf_tensor(f"{tag}_Qa{i}", [65, L], BF16)) for i in range(2)]
            Va = [es.enter_context(nc.sbuf_tensor(f"{tag}_Va{i}", [128, NB, 128], BF16)) for i in range(2)]
            r_K, r_Q, r_V = P.R(2), P.R(2), P.R(2)
            for i in range(2):
                P.op("pool", lambda e, i=i: e.memset(Ka[i][64:65, :], 1.0), writes=[r_K[i]])
                P.op("pool", lambda e, i=i: e.memset(Va[i][:, :, 64:128], 1.0), writes=[r_V[i]])
            pT = Rot(P, [es.enter_context(nc.sbuf_tensor(f"{tag}_pT{i}", [128, TT], BF16)) for i in range(3)])
            bcol = Rot(P, [es.enter_context(nc.sbuf_tensor(f"{tag}_bc{i}", [128, NB], F32)) for i in range(2)])
            rec = Rot(P, [es.enter_context(nc.sbuf_tensor(f"{tag}_rec{i}", [64, TT], F32)) for i in range(2)])
            yo = Rot(P, [es.enter_context(nc.sbuf_tensor(f"{tag}_fyo{i}", [64, TT], BF16)) for i in range(2)])
            pss = es.enter_context(nc.psum_tensor(f"{tag}_pss", [128, 4, TT], F32))
            pS = Rot(P, [pss[:, i, :] for i in range(4)])
            poo = es.enter_context(nc.psum_tensor(f"{tag}_poo", [128, 2, TT], F32))
            pO = Rot(P, [poo[:, i, :] for i in range(2)])

            def load_head(h):
                b = h % 2
                P.dma("sp", Ka[b][0:64, :], d["fk"][h * 64:(h + 1) * 64, :], writes=[r_K[b]])
                P.dma("sp", Qa[b][0:64, :], d["fq"][h * 64:(h + 1) * 64, :], writes=[r_Q[b]])
                for qb_ in range(NQ):
                    P.dma("sp", Qa[b][64:65, qb_ * TT:(qb_ + 1) * TT], rbf[h * 16 + qb_:h * 16 + qb_ + 1, :],
                          reads=[r_rbf], writes=[r_Q[b]])
                for part in range(4):
                    bs = slice(part * 16, (part + 1) * 16)
                    P.dma("sp", Va[b][:, bs, 0:64],
                          d["fv"][part * 2048:(part + 1) * 2048, h * 64:(h + 1) * 64].rearrange("(b p) d -> p b d", p=128),
                          writes=[r_V[b]])

            load_head(0)
            for h in range(4):
                b = h % 2
                if h + 1 < 4:
                    load_head(h + 1)
                K, Q, V = Ka[b], Qa[b], Va[b]
                for qb in range(NQ):
                    nkb = 4 * (qb + 1)
                    bc, bc_r = bcol.next()
                    P.op("dve", lambda e, bc=bc, nkb=nkb, qb=qb, h=h: e.tensor_scalar(
                        bc[:, 0:nkb], Ctok[:, 0:nkb, h], -1.0, totb[:, 4 * qb, h:h + 1], ALU.mult, ALU.add),
                        reads=[r_C, r_tot], writes=[bc_r])
                    po, po_r = pO.next()
                    q0 = qb * TT
                    for kb in range(nkb):
                        j = kb - 4 * qb
                        ks = slice(kb * 128, (kb + 1) * 128)
                        ps, ps_r = pS.next()
                        pt, pt_r = pT.next()
                        if j < 0:
                            lo = 0
                            fns = [_mm(ps[:, :], K[:, ks], Q[:, q0:q0 + TT], True, True)]
                        else:
                            lo = 128 * j
                            fns = [_mm(ps[:, lo:lo + 128], K[:, ks], Q[:, q0 + lo:q0 + lo + 128], True, False),
                                   _mm(ps[:, lo:lo + 128], triu[:, :], nbI[:, :], False, True)]
                            if lo + 128 < TT:
                                fns.append(_mm(ps[:, lo + 128:TT], K[:, ks], Q[:, q0 + lo + 128:q0 + TT], True, True))
                        P.group("pe", fns, reads=[r_K[b], r_Q[b], cres], writes=[ps_r])
                        P.op("act", lambda e, pt=pt, ps=ps, lo=lo, bc=bc, kb=kb: e.activation(
                            pt[:, lo:TT], ps[:, lo:TT], AF.Exp, bias=bc[:, kb:kb + 1], scale=1.0),
                            reads=[ps_r, bc_r], writes=[pt_r])
                        P.group("pe", [_mm(po[:, lo:TT], V[:, kb, :], pt[:, lo:TT], kb == 0, kb == nkb - 1)],
                                reads=[pt_r, r_V[b]], writes=[po_r])
                    rc, rc_r = rec.next()
                    P.op("dve", lambda e, rc=rc, po=po: e.reciprocal(rc[:, :], po[64:128, :]), reads=[po_r], writes=[rc_r])
                    y_t, y_r = yo.next()
                    P.op("dve", lambda e, y_t=y_t, po=po, rc=rc: e.tensor_tensor(y_t[:, :], po[0:64, :], rc[:, :], ALU.mult),
                         reads=[po_r, rc_r], writes=[y_r])
                    P.dma("sp", yT[256 + h * 64:256 + (h + 1) * 64, q0:q0 + TT], y_t[:, :], reads=[y_r], is_output=True)
        P.barrier()


def mixout_phase(P, C, x_in, yT_in, w_mo_d, x_out, ntok, tag):
    nc = P.nc
    nt = ntok // TT
    with contextlib.ExitStack() as es:
        SW = 1024
        stage = [es.enter_context(nc.sbuf_tensor(f"{tag}_st{i}", [128, SW], F32)) for i in range(2)]
        stage_res = P.R(2)
        w, w_r = load_cast_weight(P, es, f"{tag}_wmo", w_mo_d.rearrange("(k p) n -> p k n", p=128),
                                  KC, D_MODEL, stage, stage_res, ["pool", "dve", "act"], ["sp"])
        xT = [es.enter_context(nc.sbuf_tensor(f"{tag}_xT{i}", [128, KC, TT], F32)) for i in range(2)]
        xT_res = P.R(2)
        yt = [es.enter_context(nc.sbuf_tensor(f"{tag}_yt{i}", [128, KC, TT], BF16)) for i in range(2)]
        yt_res = P.R(2)
        pso = es.enter_context(nc.psum_tensor(f"{tag}_pso", [128, 4, TT], F32))
        ps = Rot(P, [pso[:, i, :] for i in range(4)])
        x_in_v = x_in.rearrange("(k p) t -> p k t", p=128)
        x_out_v = x_out.rearrange("(k p) t -> p k t", p=128)
        y_v = yT_in.rearrange("(k p) t -> p k t", p=128)
        for t in range(nt):
            b = t % 2
            ts = slice(t * TT, (t + 1) * TT)
            P.dma("sp", xT[b][:, :, :], x_in_v[:, :, ts], writes=[xT_res[b]])
            P.dma("sp", yt[b][:, :, :], y_v[:, :, ts], writes=[yt_res[b]])
            for m in range(KC):
                pt, pr = ps.next()
                fns = [_mm(pt, w[:, k, m * 128:(m + 1) * 128], yt[b][:, k, :], k == 0, k == KC - 1) for k in range(KC)]
                P.group("pe", fns, reads=[yt_res[b]] + (w_r if (t == 0 and m == 0) else []), writes=[pr])
                P.op("dve", lambda e, b=b, m=m, pt=pt: e.tensor_tensor(xT[b][:, m, :], pt, xT[b][:, m, :], ALU.add),
                     reads=[pr, xT_res[b]], writes=[xT_res[b]])
            P.dma("sp", x_out_v[:, :, ts], xT[b][:, :, :], reads=[xT_res[b]])
    P.barrier()


def _din(nc, name, shape, dt):
    return nc.dram_tensor(name, list(shape), dt, kind="ExternalInput").ap()


def _dout(nc, name, shape, dt):
    return nc.dram_tensor(name, list(shape), dt, kind="ExternalOutput").ap()


A_OUTS = (("cu", (256, TOK), BF16), ("cb", (256, TOK), BF16), ("hq", (256, TOK), BF16), ("hk", (256, TOK), BF16),
          ("hg", (256, TOK), BF16), ("hlf", (256, TOK), F32), ("hv", (TOK, 256), BF16), ("fq", (512, TOK), BF16),
          ("fk", (512, TOK), BF16), ("fv", (TOK, 512), BF16), ("flf", (8, TOK), F32))


def build_A():
    nc = bass.Bass("TRN2", target_bir_lowering=False)
    xT = _din(nc, "xT", (D_MODEL, TOK), F32)
    g1 = _din(nc, "g1", (128, KC), F32)
    w_in = _din(nc, "w_in", (D_MODEL, 2 * D_FF), F32)
    w_out = _din(nc, "w_out", (D_FF, D_MODEL), F32)
    gm = _din(nc, "gm", (128, KC), F32)
    w_mi = _din(nc, "w_mi", (D_MODEL, MIX_IN), F32)
    sm = {"lbl": _din(nc, "lbl", (128, 2, 2), F32), "qg": _din(nc, "qg", (128, 1), F32),
          "kg": _din(nc, "kg", (128, 1), F32), "fb": _din(nc, "fb", (8, 1), F32),
          "lflag": _din(nc, "lflag", (128, 1), F32)}
    x1T = _dout(nc, "x1T", (D_MODEL, TOK), F32)
    outs = {n: _dout(nc, n, shp, dt) for (n, shp, dt) in A_OUTS}
    P = Prog(nc)
    with contextlib.ExitStack() as es:
        C = Consts(P, es)
        ffn_phase(P, C, xT, x1T, g1, w_in, w_out, TOK, "f1", out_is_output=True)
        mixin_phase(P, C, x1T, gm, w_mi, sm, outs, TOK, "mi")
        P.finish()
    return nc


def build_M():
    nc = bass.Bass("TRN2", target_bir_lowering=False)
    d = {}
    for n in ("cu", "cb", "hq", "hk", "hg"):
        d[n] = _din(nc, n, (128, SEQ), BF16)
    d["hlf"] = _din(nc, "hlf", (128, SEQ), F32)
    d["hv"] = _din(nc, "hv", (SEQ, 128), BF16)
    d["fq"] = _din(nc, "fq", (256, SEQ), BF16)
    d["fk"] = _din(nc, "fk", (256, SEQ), BF16)
    d["fv"] = _din(nc, "fv", (SEQ, 256), BF16)
    d["flf_tok"] = _din(nc, "flf_tok", (128, 256), F32)
    d["flf_q"] = _din(nc, "flf_q", (64, TT), F32)
    d["cw"] = _din(nc, "cw", (128, 3), F32)
    d["hgain"] = _din(nc, "hgain", (64, 1), F32)
    yT = _dout(nc, "yT", (512, SEQ), BF16)
    P = Prog(nc)
    with contextlib.ExitStack() as es:
        C = Consts(P, es)
        mixer_phase(P, C, d, yT, "mx")
        P.finish()
    return nc


def build_C():
    nc = bass.Bass("TRN2", target_bir_lowering=False)
    x1T = _din(nc, "x1T", (D_MODEL, TOK), F32)
    yT = _din(nc, "yTf", (D_MODEL, TOK), BF16)
    w_mo = _din(nc, "w_mo", (D_MODEL, D_MODEL), F32)
    g2 = _din(nc, "g2", (128, KC), F32)
    w_in = _din(nc, "w_in", (D_MODEL, 2 * D_FF), F32)
    w_out = _din(nc, "w_out", (D_FF, D_MODEL), F32)
    x2T = _dout(nc, "x2T", (D_MODEL, TOK), F32)
    x3T = _dout(nc, "x3T", (D_MODEL, TOK), F32)
    P = Prog(nc)
    with contextlib.ExitStack() as es:
        C = Consts(P, es)
        mixout_phase(P, C, x1T, yT, w_mo, x2T, TOK, "mo")
        ffn_phase(P, C, x2T, x3T, g2, w_in, w_out, TOK, "f2", out_is_output=True)
        P.finish()
    return nc


def _g8(v):
    return np.ascontiguousarray(np.asarray(v, np.float32).reshape(KC, 128).T)


def _run(nc, in_maps):
    res = run_bass_kernel_spmd(nc, in_maps, core_ids=list(range(NCORES)))
    return res.results


def kernel(x, ffn1_norm, ffn1_w_in, ffn1_w_out, mix_norm, w_mix_in, conv_w, hgrn_lb_logits, hgrn_out_gain,
           fox_q_gain, fox_k_gain, fox_f_bias, w_mix_out, ffn2_norm, ffn2_w_in, ffn2_w_out):
    f32 = np.float32
    x = np.asarray(x, f32)
    half = SEQ // 2
    xT = [np.ascontiguousarray(x[c // 2, (c % 2) * half:(c % 2 + 1) * half, :].T) for c in range(NCORES)]
    lbl = np.ascontiguousarray(np.asarray(hgrn_lb_logits, f32).T.reshape(2, 128, DEPTH).transpose(1, 0, 2))
    ncA, ncM, ncC = build_A(), build_M(), build_C()
    for l in range(DEPTH):
        common = {"g1": _g8(ffn1_norm[l]), "w_in": np.ascontiguousarray(ffn1_w_in[l], f32),
                  "w_out": np.ascontiguousarray(ffn1_w_out[l], f32), "gm": _g8(mix_norm[l]),
                  "w_mi": np.ascontiguousarray(w_mix_in[l], f32), "lbl": lbl,
                  "qg": np.ascontiguousarray(np.tile(np.asarray(fox_q_gain[l], f32), 2)[:, None]),
                  "kg": np.ascontiguousarray(np.tile(np.asarray(fox_k_gain[l], f32), 2)[:, None]),
                  "fb": np.ascontiguousarray(np.asarray(fox_f_bias[l], f32)[:, None]),
                  "lflag": np.full((128, 1), float(l), f32)}
        rA = _run(ncA, [dict(common, xT=xT[c]) for c in range(NCORES)])
        in_M = []
        for c in range(NCORES):
            b, j = c // 2, c % 2
            a0, a1 = rA[2 * b], rA[2 * b + 1]
            m = {}
            for n in ("cu", "cb", "hq", "hk", "hg", "hlf"):
                m[n] = np.ascontiguousarray(np.concatenate([a0[n][j * 128:(j + 1) * 128], a1[n][j * 128:(j + 1) * 128]], axis=1))
            m["hv"] = np.ascontiguousarray(np.concatenate([a0["hv"][:, j * 128:(j + 1) * 128], a1["hv"][:, j * 128:(j + 1) * 128]], axis=0))
            for n in ("fq", "fk"):
                m[n] = np.ascontiguousarray(np.concatenate([a0[n][j * 256:(j + 1) * 256], a1[n][j * 256:(j + 1) * 256]], axis=1))
            m["fv"] = np.ascontiguousarray(np.concatenate([a0["fv"][:, j * 256:(j + 1) * 256], a1["fv"][:, j * 256:(j + 1) * 256]], axis=0))
            flf4 = np.concatenate([a0["flf"][j * 4:(j + 1) * 4], a1["flf"][j * 4:(j + 1) * 4]], axis=1)
            m["flf_tok"] = np.ascontiguousarray(flf4.reshape(4, SEQ // 128, 128).transpose(2, 1, 0).reshape(128, 256))
            m["flf_q"] = np.ascontiguousarray(flf4.reshape(64, TT))
            m["cw"] = np.ascontiguousarray(np.asarray(conv_w[l], f32)[:, j * 128:(j + 1) * 128].T)
            m["hgain"] = np.ascontiguousarray(np.asarray(hgrn_out_gain[l], f32)[:, None])
            in_M.append(m)
        rM = _run(ncM, in_M)
        in_C = []
        for c in range(NCORES):
            b, i = c // 2, c % 2
            y0, y1 = rM[2 * b]["yT"], rM[2 * b + 1]["yT"]
            yf = np.concatenate([y0[0:128], y1[0:128], y0[128:256], y1[128:256], y0[256:512], y1[256:512]], axis=0)
            in_C.append({"x1T": rA[c]["x1T"], "yTf": np.ascontiguousarray(yf[:, i * half:(i + 1) * half]),
                         "w_mo": np.ascontiguousarray(w_mix_out[l], f32), "g2": _g8(ffn2_norm[l]),
                         "w_in": np.ascontiguousarray(ffn2_w_in[l], f32), "w_out": np.ascontiguousarray(ffn2_w_out[l], f32)})
        rC = _run(ncC, in_C)
        xT = [rC[c]["x3T"] for c in range(NCORES)]
    out = np.empty((BATCH, SEQ, D_MODEL), f32)
    for c in range(NCORES):
        out[c // 2, (c % 2) * half:(c % 2 + 1) * half, :] = np.asarray(xT[c], f32).T
    return out
```
